# Optimizing a Trainium2 kernel written in Bass

```python
import jax, jax.numpy as jnp
from jax import lax
import numpy as np

D_MODEL = 1024
BATCH = 4
SEQ = 8192
DEPTH = 2

CHUNK = 64
Q_BLOCK = 128
D_FF = 2816
BRANCH_WIDTH = 512
N_BRANCH = 3
LRU_WIDTH = BRANCH_WIDTH
LRU_BLOCKS = 8
LRU_BLOCK = LRU_WIDTH // LRU_BLOCKS
CONV_WIDTH = 4
LRU_C = 8.0
GLA_HEADS = 4
GLA_DV = BRANCH_WIDTH // GLA_HEADS
GLA_DK = GLA_DV // 2
GLA_LOWRANK = 16
GLA_TAU = 16.0
FOX_HEADS = 8
FOX_DH = BRANCH_WIDTH // FOX_HEADS
PLE_DIM = 256
LN_EPS = 1e-5
RMS_EPS = 1e-6
DEEPNORM_ALPHA = (2 * DEPTH) ** 0.25
DEEPNORM_BETA = (8 * DEPTH) ** -0.25

SPLIT_SIZES = (
    LRU_WIDTH,
    LRU_WIDTH,
    GLA_HEADS * GLA_DK,
    GLA_HEADS * GLA_DK,
    GLA_HEADS * GLA_DV,
    GLA_LOWRANK,
    GLA_HEADS * GLA_DV,
    FOX_HEADS * FOX_DH,
    FOX_HEADS * FOX_DH,
    FOX_HEADS * FOX_DH,
    FOX_HEADS,
    N_BRANCH * D_MODEL,
)
SPLIT_POINTS = tuple(int(v) for v in np.cumsum(SPLIT_SIZES)[:-1])
D_IN = int(sum(SPLIT_SIZES))

kernel_name = 'hybrid_rglru_gla_fox_macaron_deepnorm'


def layer_norm(x, g, b):
    xf = x.astype(jnp.float32)
    mu = jnp.mean(xf, axis=-1, keepdims=True)
    var = jnp.mean(jnp.square(xf - mu), axis=-1, keepdims=True)
    y = (xf - mu) * lax.rsqrt(var + LN_EPS) * g.astype(jnp.float32) + b.astype(jnp.float32)
    return y.astype(x.dtype)


def swiglu(x, w_up, w_down):
    gate, up = jnp.split(x @ w_up, 2, axis=-1)
    return (jax.nn.silu(gate) * up) @ w_down


def linear_scan(a, b):
    def combine(left, right):
        return (left[0] * right[0], right[0] * left[1] + right[1])
    return lax.associative_scan(combine, (a, b), axis=1)[1]


def causal_depthwise_conv(x, w, b):
    y = lax.conv_general_dilated(
        x, w[:, None, :], window_strides=(1,), padding=[(CONV_WIDTH - 1, 0)],
        dimension_numbers=('NWC', 'WIO', 'NWC'), feature_group_count=x.shape[-1])
    return y + b


def rg_lru(x, wa, ba, wx, bx, lam):
    bsz, seq, width = x.shape
    f32 = jnp.float32
    xf = x.astype(f32)
    xb = xf.reshape(bsz, seq, LRU_BLOCKS, LRU_BLOCK)
    r = jax.nn.sigmoid(jnp.einsum('bsnc,ncd->bsnd', xb, wa.astype(f32)).reshape(bsz, seq, width) + ba.astype(f32))
    i = jax.nn.sigmoid(jnp.einsum('bsnc,ncd->bsnd', xb, wx.astype(f32)).reshape(bsz, seq, width) + bx.astype(f32))
    log_a = -LRU_C * r * jax.nn.softplus(-lam.astype(f32))
    a = jnp.exp(log_a)
    mult = jnp.sqrt(-jnp.expm1(2.0 * log_a))
    h = linear_scan(a, mult * (i * xf))
    return h.astype(x.dtype)


def gla(q, k, v, g_low, w_g2, b_g, norm_g, out_gate):
    bsz, seq, _ = q.shape
    nc = seq // CHUNK
    f32 = jnp.float32
    qc = q.astype(f32).reshape(bsz, nc, CHUNK, GLA_HEADS, GLA_DK) * (GLA_DK ** -0.5)
    kc = k.astype(f32).reshape(bsz, nc, CHUNK, GLA_HEADS, GLA_DK)
    vc = v.astype(f32).reshape(bsz, nc, CHUNK, GLA_HEADS, GLA_DV)
    log_alpha = jax.nn.log_sigmoid((g_low @ w_g2 + b_g).astype(f32)) / GLA_TAU
    log_alpha = log_alpha.reshape(bsz, nc, CHUNK, GLA_HEADS, GLA_DK)
    g_cum = jnp.cumsum(log_alpha, axis=2)
    g_tot = g_cum[:, :, -1]
    k_dec = kc * jnp.exp(g_tot[:, :, None] - g_cum)
    delta = jnp.einsum('bnchk,bnchv->bnhkv', k_dec, vc)
    state = linear_scan(jnp.exp(g_tot)[..., None], delta)
    o = jnp.einsum('bnchk,bnhkv->bnchv', qc, state)
    o = o * lax.rsqrt(jnp.mean(jnp.square(o), axis=-1, keepdims=True) + RMS_EPS)
    o = o.reshape(bsz, seq, GLA_HEADS * GLA_DV) * norm_g.astype(f32)
    return o.astype(q.dtype) * jax.nn.silu(out_gate)


def forgetting_attention(q, k, v, f_logit, b_f):
    bsz, seq, _ = q.shape
    nb = seq // Q_BLOCK
    f32 = jnp.float32
    scale = FOX_DH ** -0.5
    qh = q.reshape(bsz, seq, FOX_HEADS, FOX_DH)
    kh = k.reshape(bsz, seq, FOX_HEADS, FOX_DH).transpose(0, 2, 1, 3)
    vh = v.reshape(bsz, seq, FOX_HEADS, FOX_DH).transpose(0, 2, 1, 3)
    log_f = jax.nn.log_sigmoid((f_logit + b_f).astype(f32))
    f_cum = jnp.cumsum(log_f, axis=1).transpose(0, 2, 1)
    q_blk = qh.reshape(bsz, nb, Q_BLOCK, FOX_HEADS, FOX_DH).transpose(1, 0, 3, 2, 4)
    f_blk = f_cum.reshape(bsz, FOX_HEADS, nb, Q_BLOCK).transpose(2, 0, 1, 3)
    kpos = jnp.arange(seq)

    def attend(args):
        qb, fb, n = args
        logits = jnp.einsum('bhqd,bhkd->bhqk', qb, kh).astype(f32) * scale
        logits = logits + (fb[..., None] - f_cum[:, :, None, :])
        qpos = n * Q_BLOCK + jnp.arange(Q_BLOCK)
        mask = kpos[None, :] <= qpos[:, None]
        probs = jax.nn.softmax(jnp.where(mask, logits, -jnp.inf), axis=-1)
        return jnp.einsum('bhqk,bhkd->bhqd', probs.astype(vh.dtype), vh)

    out = lax.map(attend, (q_blk, f_blk, jnp.arange(nb)))
    return out.transpose(1, 0, 3, 2, 4).reshape(bsz, seq, FOX_HEADS * FOX_DH)


def hybrid_mixer(x, w_in, conv_w, conv_b, lru_wa, lru_ba, lru_wx, lru_bx, lru_lambda,
                 gla_w_g2, gla_b_g, gla_norm_g, fox_b_f, w_branch, w_out):
    bsz, seq, _ = x.shape
    (a_x, a_y, b_q, b_k, b_v, b_low, b_r,
     c_q, c_k, c_v, c_f, gate_logits) = jnp.split(x @ w_in, SPLIT_POINTS, axis=-1)
    y_a = jax.nn.gelu(a_y) * rg_lru(causal_depthwise_conv(a_x, conv_w, conv_b),
                                    lru_wa, lru_ba, lru_wx, lru_bx, lru_lambda)
    y_b = gla(b_q, b_k, b_v, b_low, gla_w_g2, gla_b_g, gla_norm_g, b_r)
    y_c = forgetting_attention(c_q, c_k, c_v, c_f, fox_b_f)
    gates = jax.nn.sigmoid(gate_logits).reshape(bsz, seq, N_BRANCH, D_MODEL)
    merged = (gates[:, :, 0] * (y_a @ w_branch[0])
              + gates[:, :, 1] * (y_b @ w_branch[1])
              + gates[:, :, 2] * (y_c @ w_branch[2]))
    return merged @ w_out


def setup_inputs(seed: int = 0) -> dict:
    key = jax.random.key(seed)
    ks = jax.random.split(key, 40)
    f32 = jnp.float32

    def nrm(k, shape, scale):
        return jax.random.normal(k, shape, f32) * scale

    def gain(k, shape):
        return 1.0 + 0.02 * jax.random.normal(k, shape, f32)

    u = jax.random.uniform(ks[12], (DEPTH, LRU_WIDTH), f32, minval=0.9, maxval=0.999)
    a0 = u ** (1.0 / LRU_C)
    lru_lambda = jnp.log(a0) - jnp.log1p(-a0)

    return {
        'x': nrm(ks[0], (BATCH, SEQ, D_MODEL), 1.0),
        'p': nrm(ks[1], (DEPTH, BATCH, SEQ, PLE_DIM), 1.0),
        'ffn1_w_up': nrm(ks[2], (DEPTH, D_MODEL, 2 * D_FF), D_MODEL ** -0.5),
        'ffn1_w_down': nrm(ks[3], (DEPTH, D_FF, D_MODEL), D_FF ** -0.5 * DEEPNORM_BETA),
        'ln1_g': gain(ks[4], (DEPTH, D_MODEL)),
        'ln1_b': nrm(ks[5], (DEPTH, D_MODEL), 0.02),
        'w_in': nrm(ks[6], (DEPTH, D_MODEL, D_IN), D_MODEL ** -0.5),
        'conv_w': nrm(ks[7], (DEPTH, CONV_WIDTH, LRU_WIDTH), CONV_WIDTH ** -0.5),
        'conv_b': nrm(ks[8], (DEPTH, LRU_WIDTH), 0.02),
        'lru_wa': nrm(ks[9], (DEPTH, LRU_BLOCKS, LRU_BLOCK, LRU_BLOCK), LRU_BLOCK ** -0.5),
        'lru_ba': nrm(ks[10], (DEPTH, LRU_WIDTH), 0.02),
        'lru_wx': nrm(ks[11], (DEPTH, LRU_BLOCKS, LRU_BLOCK, LRU_BLOCK), LRU_BLOCK ** -0.5),
        'lru_bx': nrm(ks[13], (DEPTH, LRU_WIDTH), 0.02),
        'lru_lambda': lru_lambda,
        'gla_w_g2': nrm(ks[14], (DEPTH, GLA_LOWRANK, GLA_HEADS * GLA_DK), GLA_LOWRANK ** -0.5),
        'gla_b_g': nrm(ks[15], (DEPTH, GLA_HEADS * GLA_DK), 0.02),
        'gla_norm_g': gain(ks[16], (DEPTH, GLA_HEADS * GLA_DV)),
        'fox_b_f': jax.random.uniform(ks[17], (DEPTH, FOX_HEADS), f32, minval=1.0, maxval=4.0),
        'w_branch': nrm(ks[18], (DEPTH, N_BRANCH, BRANCH_WIDTH, D_MODEL), BRANCH_WIDTH ** -0.5),
        'w_out': nrm(ks[19], (DEPTH, D_MODEL, D_MODEL), D_MODEL ** -0.5 * DEEPNORM_BETA),
        'ln2_g': gain(ks[20], (DEPTH, D_MODEL)),
        'ln2_b': nrm(ks[21], (DEPTH, D_MODEL), 0.02),
        'ffn2_w_up': nrm(ks[22], (DEPTH, D_MODEL, 2 * D_FF), D_MODEL ** -0.5),
        'ffn2_w_down': nrm(ks[23], (DEPTH, D_FF, D_MODEL), D_FF ** -0.5 * DEEPNORM_BETA),
        'ln3_g': gain(ks[24], (DEPTH, D_MODEL)),
        'ln3_b': nrm(ks[25], (DEPTH, D_MODEL), 0.02),
        'ple_w_proj': nrm(ks[26], (DEPTH, PLE_DIM, D_MODEL), PLE_DIM ** -0.5 * DEEPNORM_BETA),
        'ple_w_gate': nrm(ks[27], (DEPTH, D_MODEL, D_MODEL), D_MODEL ** -0.5),
        'ple_b_gate': nrm(ks[28], (DEPTH, D_MODEL), 0.02),
        'ln4_g': gain(ks[29], (DEPTH, D_MODEL)),
        'ln4_b': nrm(ks[30], (DEPTH, D_MODEL), 0.02),
    }


def reference(x, p, ffn1_w_up, ffn1_w_down, ln1_g, ln1_b, w_in, conv_w, conv_b,
              lru_wa, lru_ba, lru_wx, lru_bx, lru_lambda, gla_w_g2, gla_b_g, gla_norm_g,
              fox_b_f, w_branch, w_out, ln2_g, ln2_b, ffn2_w_up, ffn2_w_down, ln3_g, ln3_b,
              ple_w_proj, ple_w_gate, ple_b_gate, ln4_g, ln4_b):
    for i in range(DEPTH):
        x = layer_norm(DEEPNORM_ALPHA * x + 0.5 * swiglu(x, ffn1_w_up[i], ffn1_w_down[i]),
                       ln1_g[i], ln1_b[i])
        mix = hybrid_mixer(x, w_in[i], conv_w[i], conv_b[i], lru_wa[i], lru_ba[i], lru_wx[i],
                           lru_bx[i], lru_lambda[i], gla_w_g2[i], gla_b_g[i], gla_norm_g[i],
                           fox_b_f[i], w_branch[i], w_out[i])
        x = layer_norm(DEEPNORM_ALPHA * x + mix, ln2_g[i], ln2_b[i])
        x = layer_norm(DEEPNORM_ALPHA * x + 0.5 * swiglu(x, ffn2_w_up[i], ffn2_w_down[i]),
                       ln3_g[i], ln3_b[i])
        pe = p[i] @ ple_w_proj[i]
        x = layer_norm(DEEPNORM_ALPHA * x + jax.nn.sigmoid(x @ ple_w_gate[i] + ple_b_gate[i]) * pe,
                       ln4_g[i], ln4_b[i])
    return x
```

```python
import contextlib
import numpy as np
import ml_dtypes
import concourse.bass as bass
import concourse.mybir as mybir
from concourse.alu_op_type import AluOpType as ALU
from concourse.bass_utils import run_bass_kernel_spmd

AF = mybir.ActivationFunctionType
F32 = mybir.dt.float32
BF16 = mybir.dt.bfloat16
NPBF = ml_dtypes.bfloat16

D = 1024
DFF = 2816
SEQ = 8192
NB = 4
DEPTH = 2
ALPHA = float((2 * DEPTH) ** 0.25)
LN_EPS = 1e-5
TT = 512
TLOC = 4096
ENGS = ['pe', 'act', 'dve', 'pool', 'sp']


class Res:
    __slots__ = ('name', 'w', 'rs', 'dcount', 'semkey')

    def __init__(self, name):
        self.name = name
        self.w = None
        self.rs = []
        self.dcount = 0
        self.semkey = 'D:' + name


class Prog:
    def __init__(self, nc):
        self.nc = nc
        self.q = {e: [] for e in ENGS}
        self.cnt = {e: 0 for e in ENGS}
        self.seen = {e: {} for e in ENGS}
        self.semkeys = set('E:' + e for e in ENGS if e != 'sp')
        self.dres = {}
        self.stack = contextlib.ExitStack()
        self.nres = 0
        self.psum_ring = []
        self.psum_i = 0

    def res(self, name=None):
        self.nres += 1
        return Res(f"{name or 'r'}{self.nres}")

    def sbuf(self, name, shape, dtype):
        return self.stack.enter_context(self.nc.sbuf_tensor(name, list(shape), dtype))

    def psum(self, name, shape, dtype=F32):
        return self.stack.enter_context(self.nc.psum_tensor(name, list(shape), dtype))

    def init_psum(self, n=8):
        for i in range(n):
            self.psum_ring.append((self.psum(f"ps{i}", [128, 512]), self.res(f"ps{i}")))

    def next_psum(self):
        r = self.psum_ring[self.psum_i % len(self.psum_ring)]
        self.psum_i += 1
        return r

    def _deps(self, e, reads, writes):
        deps = {}

        def add(t):
            if t is None:
                return
            k, v = t
            if deps.get(k, 0) < v:
                deps[k] = v
        for r in reads:
            add(r.w)
        for w in writes:
            add(w.w)
            for t in w.rs:
                add(t)
        waits = []
        seen = self.seen[e]
        for k, v in deps.items():
            if k == 'E:pe' and e == 'pe':
                continue
            if seen.get(k, 0) >= v:
                continue
            seen[k] = v
            waits.append((k, v))
        return waits

    def op(self, e, fn, reads=(), writes=()):
        waits = self._deps(e, reads, writes)
        self.cnt[e] += 1
        tok = ('E:' + e, self.cnt[e])
        self.q[e].append((waits, fn, ('E:' + e, 1)))
        for r in reads:
            r.rs.append(tok)
            if len(r.rs) > 64:
                r.rs = self._compact(r.rs)
        for w in writes:
            w.w = tok
            w.rs = []
        return tok

    @staticmethod
    def _compact(toks):
        best = {}
        for k, v in toks:
            if best.get(k, 0) < v:
                best[k] = v
        return list(best.items())

    def dma(self, e, parts, dst, reads=(), writes=None):
        writes = [dst] if writes is None else writes
        waits = self._deps(e, reads, writes)
        self.semkeys.add(dst.semkey)
        self.dres[dst.semkey] = dst
        first = True
        for p in parts:
            out_ap, in_ap = p[0], p[1]
            dst.dcount += 16
            fn = (lambda eng, o=out_ap, i=in_ap: eng.dma_start(out=o, in_=i))
            self.q[e].append((waits if first else [], fn, (dst.semkey, 16)))
            first = False
        tok = (dst.semkey, dst.dcount)
        for r in reads:
            r.rs.append(tok)
        for w in writes:
            w.w = tok
            w.rs = []
        return tok

    def barrier(self):
        allt = [('E:' + e, self.cnt[e]) for e in ENGS if e != 'sp' and self.cnt[e] > 0]
        allt += [(k, r.dcount) for k, r in self.dres.items() if r.dcount > 0]
        for e in ENGS:
            waits = []
            for k, v in allt:
                if k == 'E:' + e:
                    continue
                if self.seen[e].get(k, 0) >= v:
                    continue
                self.seen[e][k] = v
                waits.append((k, v))
            if waits:
                self.q[e].append((waits, None, None))

    def mm(self, out, lhsT, rhs, start, stop, reads, writes, **kw):
        return self.op('pe', lambda eng: eng.matmul(out, lhsT, rhs, start=start, stop=stop, **kw), reads, writes)

    def act(self, out, in_, func, reads, writes, bias=None, scale=None):
        def fn(eng):
            k = {}
            if bias is not None:
                k['bias'] = bias
            if scale is not None:
                k['scale'] = scale
            return eng.activation(out=out, in_=in_, func=func, **k)
        return self.op('act', fn, reads, writes)

    def tt(self, e, out, in0, in1, op, reads, writes):
        return self.op(e, lambda eng: eng.tensor_tensor(out=out, in0=in0, in1=in1, op=op), reads, writes)

    def ts(self, e, out, in0, s1, op0, reads, writes, s2=None, op1=None):
        def fn(eng):
            if op1 is None:
                return eng.tensor_scalar(out=out, in0=in0, scalar1=s1, scalar2=None, op0=op0)
            return eng.tensor_scalar(out=out, in0=in0, scalar1=s1, scalar2=s2, op0=op0, op1=op1)
        return self.op(e, fn, reads, writes)

    def stt(self, e, out, in0, scalar, in1, op0, op1, reads, writes):
        return self.op(e, lambda eng: eng.scalar_tensor_tensor(out=out, in0=in0, scalar=scalar, in1=in1, op0=op0, op1=op1), reads, writes)

    def copy(self, e, out, in_, reads, writes):
        if e == 'act':
            return self.op(e, lambda eng: eng.copy(out=out, in_=in_), reads, writes)
        return self.op(e, lambda eng: eng.tensor_copy(out=out, in_=in_), reads, writes)

    def memset(self, e, ap, val, writes):
        return self.op(e, lambda eng: eng.memset(ap, val), (), writes)

    def emit(self, final=()):
        nc = self.nc
        sems = {}
        for k in sorted(self.semkeys):
            sems[k] = self.stack.enter_context(nc.semaphore(k.replace(':', '_')))
        engmap = {'pe': 'tensor', 'act': 'scalar', 'dve': 'vector', 'pool': 'gpsimd', 'sp': 'sync'}
        block = self.stack.enter_context(nc.Block())
        for e in ENGS:
            q = self.q[e]
            extra = [(r.semkey, r.dcount) for r in final] if e == 'sp' else []

            def body(eng, q=q, extra=extra):
                for waits, fn, inc in q:
                    for k, v in waits:
                        eng.wait_ge(sems[k], v)
                    if fn is None:
                        continue
                    ins = fn(eng)
                    if inc is not None:
                        ins.then_inc(sems[inc[0]], inc[1])
                for k, v in extra:
                    eng.wait_ge(sems[k], v)
            getattr(block, engmap[e])(body)
        self.stack.close()


class Arena:
    def __init__(self, P, words):
        self.t = P.sbuf("arena", [128, words], F32)
        self.words = words
        self.off = 0

    def reset(self):
        self.off = 0

    def f32(self, n):
        a = self.t[:, self.off:self.off + n]
        self.off += n
        assert self.off <= self.words, f"arena overflow {self.off}"
        return a

    def bf16(self, n):
        w = (n + 1) // 2
        a = self.t[:, self.off:self.off + w].bitcast(BF16)
        self.off += w
        assert self.off <= self.words, f"arena overflow {self.off}"
        return a[:, 0:n]


class TLCtx:
    def __init__(self, P, nc, arena_words=41500):
        self.P = P
        self.nc = nc
        A = self.A = Arena(P, arena_words)
        self.fb = [(A.f32(8 * TT).rearrange("p (k n) -> p k n", k=8), P.res('fb')) for _ in range(3)]
        self.bb = [(A.bf16(8 * TT).rearrange("p (k n) -> p k n", k=8), P.res('bb')) for _ in range(3)]
        self.fi = 0
        self.bi = 0
        self.h = A.bf16(22 * TT).rearrange("p (k n) -> p k n", k=22)
        self.r_h = P.res('h')
        self.sg = [(A.f32(TT), P.res('sg')) for _ in range(2)]
        self.mean, self.r_mean = A.f32(TT), P.res('mean')
        self.m2, self.r_m2 = A.f32(TT), P.res('m2')
        self.var, self.r_var = A.f32(TT), P.res('var')
        self.sd, self.r_sd = A.f32(TT), P.res('sd')
        self.rstd, self.r_rstd = A.f32(TT), P.res('rstd')
        self.z = [(A.f32(TT), P.res('z')) for _ in range(2)]
        self.z2 = [(A.f32(TT), P.res('z2')) for _ in range(2)]
        self.zi = 0
        self.ws = [(A.bf16(22 * 128), P.res('ws')) for _ in range(3)]
        self.wi = 0
        self.ybf, self.r_ybf = A.bf16(12 * TT).rearrange("p (k n) -> p k n", k=12), P.res('ybf')
        self.pf, self.r_pf = A.f32(2 * TT).rearrange("p (k n) -> p k n", k=2), P.res('pf')
        self.pb, self.r_pb = A.bf16(2 * TT).rearrange("p (k n) -> p k n", k=2), P.res('pb')
        self.gt = [(A.f32(TT), P.res('gt')) for _ in range(2)]
        self.tmp = [(A.f32(TT), P.res('tmp')) for _ in range(2)]
        self.gi = 0
        self.onesm, self.r_const = A.bf16(128), P.res('const')
        self.cst = A.f32(8)
        self.vec = A.f32(80)
        self.r_vec = P.res('vec')
        P.memset('dve', self.onesm, 1.0 / 1024.0, [self.r_const])
        P.memset('dve', self.cst[:, 0:1], 4.0 * LN_EPS, [self.r_const])
        P.memset('dve', self.cst[:, 1:2], LN_EPS, [self.r_const])
        P.memset('dve', self.cst[:, 2:3], 1.0, [self.r_const])

    def fbuf(self):
        r = self.fb[self.fi % 3]
        self.fi += 1
        return r

    def bbuf(self):
        r = self.bb[self.bi % 3]
        self.bi += 1
        return r

    def wload(self, dram_ap, n, src_res):
        buf, r = self.ws[self.wi % 3]
        self.wi += 1
        self.P.dma('sp', [(buf[:, 0:n], dram_ap)], r, reads=[src_res])
        return buf, r


def layer_norm(C, w, r_w, eps_col, gcol, out_f, r_of, out_b, r_ob):
    P = C.P
    wsq = C.h[:, 0:8, :]
    for k in range(8):
        P.copy('pool', out_b[:, k, :], w[:, k, :], [r_w], [r_ob])
        P.act(wsq[:, k, :], w[:, k, :], AF.Square, [r_w], [C.r_h])
    mps, r_m = P.next_psum()
    qps, r_q = P.next_psum()
    for k in range(8):
        P.mm(mps[:], C.onesm, out_b[:, k, :], k == 0, k == 7, [C.r_const, r_ob], [r_m])
    for k in range(8):
        P.mm(qps[:], C.onesm, wsq[:, k, :], k == 0, k == 7, [C.r_const, C.r_h], [r_q])
    P.copy('act', C.mean, mps[:], [r_m], [C.r_mean])
    P.tt('pool', C.m2, C.mean, C.mean, ALU.mult, [C.r_mean], [C.r_m2])
    P.tt('dve', C.var, qps[:], C.m2, ALU.subtract, [r_q, C.r_m2], [C.r_var])
    P.act(C.sd, C.var, AF.Sqrt, [C.r_var, C.r_const], [C.r_sd], bias=C.cst[:, eps_col:eps_col + 1])
    P.op('dve', lambda eng: eng.reciprocal(out=C.rstd, in_=C.sd), [C.r_sd], [C.r_rstd])
    for k in range(8):
        z, r_z = C.z[C.zi % 2]
        z2, r_z2 = C.z2[C.zi % 2]
        C.zi += 1
        P.tt('dve', z, w[:, k, :], C.mean, ALU.subtract, [r_w, C.r_mean], [r_z])
        P.tt('pool', z2, z, C.rstd, ALU.mult, [r_z, C.r_rstd], [r_z2])
        P.act(out_f[:, k, :], z2, AF.Identity, [r_z2, C.r_vec], [r_of],
              scale=C.vec[:, gcol + k:gcol + k + 1], bias=C.vec[:, gcol + 8 + k:gcol + 9 + k])
        P.copy('pool', out_b[:, k, :], out_f[:, k, :], [r_of], [r_ob])


def ffn_resid(C, xb, r_xb, xf, r_xf, wup, r_wup, wdn, r_wdn, wout, r_wout):
    P = C.P
    for j in range(22):
        wt, r_wt = C.wload(wup[j], 2048, r_wup)
        wv = wt[:, 0:2048].rearrange("p (a k c) -> p a k c", a=2, k=8)
        gps, r_g = P.next_psum()
        ups, r_u = P.next_psum()
        for k in range(8):
            P.mm(gps[:], wv[:, 0, k, :], xb[:, k, :], k == 0, k == 7, [r_wt, r_xb], [r_g])
        for k in range(8):
            P.mm(ups[:], wv[:, 1, k, :], xb[:, k, :], k == 0, k == 7, [r_wt, r_xb], [r_u])
        sg, r_sg = C.sg[j % 2]
        P.act(sg, gps[:], AF.Silu, [r_g], [r_sg])
        P.tt('dve', C.h[:, j, :], sg, ups[:], ALU.mult, [r_sg, r_u], [C.r_h])
    for m in range(8):
        wt, r_wt = C.wload(wdn[m], 2816, r_wdn)
        wv = wt[:, 0:2816].rearrange("p (j c) -> p j c", j=22)
        yps, r_y = P.next_psum()
        for j in range(22):
            P.mm(yps[:], wv[:, j, :], C.h[:, j, :], j == 0, j == 21, [r_wt, C.r_h], [r_y])
        P.stt('dve', wout[:, m, :], xf[:, m, :], 2.0 * ALPHA, yps[:], ALU.mult, ALU.add, [r_xf, r_y], [r_wout])


def tile_view(d_ap, t):
    return d_ap.rearrange("(k p) n -> p k n", p=128)[:, :, t * TT:(t + 1) * TT]


def phase_A_tile(C, t, xin, r_xin, W, x1f_d, r_x1f, x1b_d, r_x1b, gcol):
    P = C.P
    xf, r_xf = C.fbuf()
    P.dma('sp', [(xf, tile_view(xin, t))], r_xf, reads=[r_xin])
    xb, r_xb = C.bbuf()
    for k in range(8):
        P.copy('pool', xb[:, k, :], xf[:, k, :], [r_xf], [r_xb])
    wt_, r_wt_ = C.fbuf()
    ffn_resid(C, xb, r_xb, xf, r_xf, W['up'], W['r'], W['dn'], W['r'], wt_, r_wt_)
    of, r_of = C.fbuf()
    ob, r_ob = C.bbuf()
    layer_norm(C, wt_, r_wt_, 0, gcol, of, r_of, ob, r_ob)
    if x1f_d is not None:
        P.dma('act', [(tile_view(x1f_d, t), of)], r_x1f, reads=[r_of])
    if x1b_d is not None:
        P.dma('act', [(tile_view(x1b_d, t), ob)], r_x1b, reads=[r_ob])
    return of, r_of, ob, r_ob


def phase_C_tile(C, t, x1f_d, r_x1f, x1b_d, r_x1b, y_d, r_y, p_d, r_p, W, out_d, r_out):
    P = C.P
    x1f, r_x1 = C.fbuf()
    P.dma('sp', [(x1f, tile_view(x1f_d, t))], r_x1, reads=[r_x1f])
    x1b, r_x1bs = C.bbuf()
    P.dma('sp', [(x1b, tile_view(x1b_d, t))], r_x1bs, reads=[r_x1b])
    P.dma('sp', [(C.ybf, tile_view(y_d, t))], C.r_ybf, reads=[r_y])
    P.dma('sp', [(C.pf, tile_view(p_d, t))], C.r_pf, reads=[r_p])
    for k in range(2):
        P.copy('pool', C.pb[:, k, :], C.pf[:, k, :], [C.r_pf], [C.r_pb])
    rw = W['r']
    mg, r_mg = C.fbuf()
    mgb, r_mgb = C.bbuf()
    for m in range(8):
        for br in range(3):
            wt, r_wt = C.wload(W['gate'][br * 8 + m], 1024, rw)
            wv = wt[:, 0:1024].rearrange("p (k c) -> p k c", k=8)
            gps, r_g = P.next_psum()
            for k in range(8):
                P.mm(gps[:], wv[:, k, :], x1b[:, k, :], k == 0, k == 7, [r_wt, r_x1bs], [r_g])
            wt2, r_wt2 = C.wload(W['br'][br * 8 + m], 512, rw)
            wv2 = wt2[:, 0:512].rearrange("p (k c) -> p k c", k=4)
            bps, r_b = P.next_psum()
            for k in range(4):
                P.mm(bps[:], wv2[:, k, :], C.ybf[:, br * 4 + k, :], k == 0, k == 3, [r_wt2, C.r_ybf], [r_b])
            gt, r_gt = C.gt[C.gi % 2]
            P.act(gt, gps[:], AF.Sigmoid, [r_g], [r_gt])
            if br == 0:
                P.tt('dve', mg[:, m, :], gt, bps[:], ALU.mult, [r_gt, r_b], [r_mg])
            else:
                tm, r_tm = C.tmp[C.gi % 2]
                P.tt('dve', tm, gt, bps[:], ALU.mult, [r_gt, r_b], [r_tm])
                P.tt('pool', mg[:, m, :], mg[:, m, :], tm, ALU.add, [r_tm, r_mg], [r_mg])
            C.gi += 1
        P.copy('pool', mgb[:, m, :], mg[:, m, :], [r_mg], [r_mgb])
    w2, r_w2 = C.fbuf()
    for m in range(8):
        wt, r_wt = C.wload(W['out'][m], 1024, rw)
        wv = wt[:, 0:1024].rearrange("p (k c) -> p k c", k=8)
        ops_, r_o = P.next_psum()
        for k in range(8):
            P.mm(ops_[:], wv[:, k, :], mgb[:, k, :], k == 0, k == 7, [r_wt, r_mgb], [r_o])
        P.stt('dve', w2[:, m, :], x1f[:, m, :], ALPHA, ops_[:], ALU.mult, ALU.add, [r_x1, r_o], [r_w2])
    x2f, r_x2f = C.fbuf()
    x2b, r_x2b = C.bbuf()
    layer_norm(C, w2, r_w2, 1, 16, x2f, r_x2f, x2b, r_x2b)
    w3, r_w3 = C.fbuf()
    ffn_resid(C, x2b, r_x2b, x2f, r_x2f, W['up2'], rw, W['dn2'], rw, w3, r_w3)
    x3f, r_x3f = C.fbuf()
    x3b, r_x3b = C.bbuf()
    layer_norm(C, w3, r_w3, 0, 32, x3f, r_x3f, x3b, r_x3b)
    w4, r_w4 = C.fbuf()
    for m in range(8):
        wt, r_wt = C.wload(W['pg'][m], 1024, rw)
        wv = wt[:, 0:1024].rearrange("p (k c) -> p k c", k=8)
        gps, r_g = P.next_psum()
        for k in range(8):
            P.mm(gps[:], wv[:, k, :], x3b[:, k, :], k == 0, k == 7, [r_wt, r_x3b], [r_g])
        wt2, r_wt2 = C.wload(W['pp'][m], 256, rw)
        wv2 = wt2[:, 0:256].rearrange("p (k c) -> p k c", k=2)
        pps, r_pp = P.next_psum()
        for k in range(2):
            P.mm(pps[:], wv2[:, k, :], C.pb[:, k, :], k == 0, k == 1, [r_wt2, C.r_pb], [r_pp])
        gt, r_gt = C.gt[C.gi % 2]
        C.gi += 1
        P.act(gt, gps[:], AF.Sigmoid, [r_g, C.r_vec], [r_gt], bias=C.vec[:, 64 + m:65 + m])
        tm, r_tm = C.tmp[C.gi % 2]
        P.tt('dve', tm, gt, pps[:], ALU.mult, [r_gt, r_pp], [r_tm])
        P.stt('dve', w4[:, m, :], x3f[:, m, :], ALPHA, tm, ALU.mult, ALU.add, [r_x3f, r_tm], [r_w4])
    of, r_of = C.fbuf()
    ob, r_ob = C.bbuf()
    layer_norm(C, w4, r_w4, 1, 48, of, r_of, ob, r_ob)
    if out_d is not None:
        P.dma('act', [(tile_view(out_d, t), of)], r_out, reads=[r_of])
    return of, r_of, ob, r_ob


def lhsT_layout(W):
    kc, jn = W.shape[0] // 128, W.shape[1] // 128
    return np.ascontiguousarray(W.reshape(kc, 128, jn, 128).transpose(2, 1, 0, 3)).reshape(-1)


def up_layout(W):
    return np.ascontiguousarray(W.reshape(8, 128, 2, 22, 128).transpose(3, 1, 2, 0, 4)).reshape(-1)


def vec_cols(v):
    return np.ascontiguousarray(v.reshape(-1, 128).T)


def tl_weights_host(inp, l, which):
    w = {}
    if 'A' in which:
        w['up'] = up_layout(inp['ffn1_w_up'][l])
        w['dn'] = lhsT_layout(inp['ffn1_w_down'][l])
    if 'C' in which:
        w['gate'] = lhsT_layout(inp['w_in'][l][:, 4120:7192])
        w['br'] = np.concatenate([lhsT_layout(inp['w_branch'][l][b]) for b in range(3)])
        w['out'] = lhsT_layout(inp['w_out'][l])
        w['up2'] = up_layout(inp['ffn2_w_up'][l])
        w['dn2'] = lhsT_layout(inp['ffn2_w_down'][l])
        w['pg'] = lhsT_layout(inp['ple_w_gate'][l])
        w['pp'] = lhsT_layout(inp['ple_w_proj'][l])
    return w


def tl_vec_host(inp, l):
    cols = []
    for n in ['ln1', 'ln2', 'ln3', 'ln4']:
        cols.append(vec_cols(inp[n + '_g'][l]))
        cols.append(vec_cols(inp[n + '_b'][l]))
    cols.append(vec_cols(inp['ple_b_gate'][l]))
    cols.append(np.zeros((128, 8), np.float32))
    return np.ascontiguousarray(np.concatenate(cols, axis=1))


W_E = {'up': 2048, 'dn': 2816, 'gate': 1024, 'br': 512, 'out': 1024, 'up2': 2048, 'dn2': 2816, 'pg': 1024, 'pp': 256}
W_N = {'up': 1024 * 5632, 'dn': 2816 * 1024, 'gate': 1024 * 3072, 'br': 3 * 512 * 1024, 'out': 1024 * 1024,
       'up2': 1024 * 5632, 'dn2': 2816 * 1024, 'pg': 1024 * 1024, 'pp': 256 * 1024}


def declare_tl_weights(nc, P, names, tag):
    W = {'r': P.res('wcast' + tag)}
    parts = []
    for n in names:
        N, E = W_N[n], W_E[n]
        wf = nc.dram_tensor(f"wf_{tag}_{n}", [N], F32, kind="ExternalInput").ap()
        wb = nc.dram_tensor(f"wb_{tag}_{n}", [N], BF16, kind="Internal").ap()
        parts.append((wb.rearrange("(r c) -> r c", c=1024), wf.rearrange("(r c) -> r c", c=1024)))
        W[n] = wb.rearrange("(j p e) -> j p e", p=128, e=E)
    P.dma('pool', parts, W['r'])
    return W


def build_A(l_tag="A"):
    nc = bass.Bass("TRN2", target_bir_lowering=False)
    P = Prog(nc)
    P.init_psum(8)
    xin = nc.dram_tensor("xin", [D, TLOC], F32, kind="ExternalInput").ap()
    vec = nc.dram_tensor("vec", [128, 80], F32, kind="ExternalInput").ap()
    x1f = nc.dram_tensor("x1f", [D, TLOC], F32, kind="ExternalOutput").ap()
    x1b = nc.dram_tensor("x1b", [D, TLOC], BF16, kind="ExternalOutput").ap()
    C = TLCtx(P, nc)
    W = declare_tl_weights(nc, P, ['up', 'dn'], 'a')
    P.dma('sp', [(C.vec, vec)], C.r_vec)
    r_xin, r_x1f, r_x1b = P.res('xin'), P.res('x1f'), P.res('x1b')
    for t in range(TLOC // TT):
        phase_A_tile(C, t, xin, r_xin, W, x1f, r_x1f, x1b, r_x1b, 0)
    P.emit(final=[r_x1f, r_x1b])
    return nc


def run_A(inp, l, xT_cores):
    nc = build_A()
    wh = tl_weights_host(inp, l, 'A')
    vec = tl_vec_host(inp, l)
    in_maps = []
    for c in range(len(xT_cores)):
        m = {"xin": xT_cores[c], "vec": vec}
        for n, a in wh.items():
            m[f"wf_a_{n}"] = a
        in_maps.append(m)
    res = run_bass_kernel_spmd(nc, in_maps, core_ids=list(range(len(xT_cores))))
    return [r["x1f"] for r in res.results], [r["x1b"] for r in res.results]


def build_C(with_next_A):
    nc = bass.Bass("TRN2", target_bir_lowering=False)
    P = Prog(nc)
    P.init_psum(8)
    x1f = nc.dram_tensor("x1f", [D, TLOC], F32, kind="ExternalInput").ap()
    x1b = nc.dram_tensor("x1b", [D, TLOC], BF16, kind="ExternalInput").ap()
    yin = nc.dram_tensor("yin", [1536, TLOC], BF16, kind="ExternalInput").ap()
    pin = nc.dram_tensor("pin", [256, TLOC], F32, kind="ExternalInput").ap()
    vec = nc.dram_tensor("vec", [128, 80], F32, kind="ExternalInput").ap()
    C = TLCtx(P, nc)
    W = declare_tl_weights(nc, P, ['gate', 'br', 'out', 'up2', 'dn2', 'pg', 'pp'], 'c')
    P.dma('sp', [(C.vec, vec)], C.r_vec)
    r_in = P.res('cin')
    fin = []
    if not with_next_A:
        xo = nc.dram_tensor("xout", [D, TLOC], F32, kind="ExternalOutput").ap()
        r_xo = P.res('xo')
        for t in range(TLOC // TT):
            phase_C_tile(C, t, x1f, r_in, x1b, r_in, yin, r_in, pin, r_in, W, xo, r_xo)
        fin = [r_xo]
    else:
        xo = nc.dram_tensor("xmid", [D, TLOC], F32, kind="Internal").ap()
        r_xo = P.res('xo')
        for t in range(TLOC // TT):
            phase_C_tile(C, t, x1f, r_in, x1b, r_in, yin, r_in, pin, r_in, W, xo, r_xo)
        vec2 = nc.dram_tensor("vec2", [128, 80], F32, kind="ExternalInput").ap()
        W2 = declare_tl_weights(nc, P, ['up', 'dn'], 'a')
        nx1f = nc.dram_tensor("nx1f", [D, TLOC], F32, kind="ExternalOutput").ap()
        nx1b = nc.dram_tensor("nx1b", [D, TLOC], BF16, kind="ExternalOutput").ap()
        P.barrier()
        P.dma('sp', [(C.vec, vec2)], C.r_vec)
        r_nf, r_nb = P.res('nx1f'), P.res('nx1b')
        for t in range(TLOC // TT):
            phase_A_tile(C, t, xo, r_xo, W2, nx1f, r_nf, nx1b, r_nb, 0)
        fin = [r_nf, r_nb]
    P.emit(final=fin)
    return nc


def run_C(inp, l, x1f_c, x1b_c, y_c, pT_c, with_next_A):
    nc = build_C(with_next_A)
    wh = tl_weights_host(inp, l, 'C')
    vec = tl_vec_host(inp, l)
    if with_next_A:
        wh2 = tl_weights_host(inp, l + 1, 'A')
        vec2 = tl_vec_host(inp, l + 1)
    in_maps = []
    n = len(x1f_c)
    for c in range(n):
        m = {"x1f": x1f_c[c], "x1b": x1b_c[c], "yin": y_c[c], "pin": pT_c[c], "vec": vec}
        for k, a in wh.items():
            m[f"wf_c_{k}"] = a
        if with_next_A:
            m["vec2"] = vec2
            for k, a in wh2.items():
                m[f"wf_a_{k}"] = a
        in_maps.append(m)
    res = run_bass_kernel_spmd(nc, in_maps, core_ids=list(range(n)))
    if with_next_A:
        return [r["nx1f"] for r in res.results], [r["nx1b"] for r in res.results]
    return [r["xout"] for r in res.results]


NFM = 1428
NTM = 640
FM_OFF = {'ax': 0, 'ay': 256, 'gq': 512, 'glow': 640, 'gr': 656, 'fq': 912, 'fk': 1168, 'ff': 1424}


def mixer_host(inp, l, g):
    w = inp['w_in'][l]
    sl = lambda o, n: w[:, o:o + n]
    fm = np.concatenate([sl(g * 256, 256), sl(512 + g * 256, 256), sl(1024 + g * 128, 128), sl(2048, 16),
                         sl(2064 + g * 256, 256), sl(2576 + g * 256, 256), sl(3088 + g * 256, 256), sl(4112 + g * 4, 4)], axis=1)
    tm = np.concatenate([sl(1280 + g * 128, 128), sl(1536 + g * 256, 256), sl(3600 + g * 256, 256)], axis=1)
    m = {}
    m['wfm'] = np.ascontiguousarray(fm.reshape(8, 128, NFM))
    m['wtm'] = np.ascontiguousarray(tm.reshape(8, 128, NTM))
    m['lruw'] = np.ascontiguousarray(np.stack([inp['lru_wa'][l][g * 4:(g + 1) * 4], inp['lru_wx'][l][g * 4:(g + 1) * 4]]))
    m['wg2'] = np.ascontiguousarray(inp['gla_w_g2'][l][:, g * 128:(g + 1) * 128])
    m['bg'] = np.ascontiguousarray(inp['gla_b_g'][l][g * 128:(g + 1) * 128].reshape(1, 128))
    v = np.zeros((128, 32), np.float32)
    ch = lambda a: vec_cols(a[g * 256:(g + 1) * 256])
    for k in range(4):
        v[:, 2 * k:2 * k + 2] = ch(inp['conv_w'][l][k])
    v[:, 8:10] = ch(inp['conv_b'][l])
    v[:, 10:12] = ch(inp['lru_ba'][l])
    v[:, 12:14] = ch(inp['lru_bx'][l])
    v[:, 14:16] = ch(inp['lru_lambda'][l])
    v[:, 16:18] = ch(inp['gla_norm_g'][l])
    v[0:4, 18] = inp['fox_b_f'][l][g * 4:(g + 1) * 4]
    m['vecm'] = v
    return m


def build_B(stop=None):
    nc = bass.Bass("TRN2", target_bir_lowering=False)
    P = Prog(nc)
    S = SEQ
    pst = [P.psum(f"ps{i}", [128, 512]) for i in range(8)]
    psr = [P.res(f"ps{i}") for i in range(8)]
    dt_in = lambda n, s, d=F32: nc.dram_tensor(n, s, d, kind="ExternalInput").ap()
    dt_sc = lambda n, s, d: nc.dram_tensor(n, s, d, kind="Internal").ap()
    x1b = dt_in("x1b", [D, S], BF16)
    wfm = dt_in("wfm", [8, 128, NFM])
    wtm = dt_in("wtm", [8, 128, NTM])
    lruw = dt_in("lruw", [2, 4, 64, 64])
    wg2 = dt_in("wg2", [16, 128])
    bg = dt_in("bg", [1, 128])
    vecm = dt_in("vecm", [128, 32])
    yout = nc.dram_tensor("yout", [768, S], BF16, kind="ExternalOutput").ap()
    axT = dt_sc("axT", [256, S], F32)
    ayT = dt_sc("ayT", [256, S], F32)
    srT = dt_sc("srT", [256, S], F32)
    gqT = dt_sc("gqT", [128, S], BF16)
    lowT = dt_sc("lowT", [16, S], BF16)
    fqT = dt_sc("fqT", [256, S], BF16)
    fkT = dt_sc("fkT", [256, S], BF16)
    fT = dt_sc("fT", [4, S], F32)
    c8d = dt_sc("c8d", [4, S], BF16)
    gk_tm = dt_sc("gk_tm", [S, 128], F32)
    gv_tm = dt_sc("gv_tm", [S, 256], BF16)
    fv_tm = dt_sc("fv_tm", [S, 256], BF16)
    R = {n: P.res(n) for n in ['axT', 'ayT', 'srT', 'gqT', 'lowT', 'fqT', 'fkT', 'fT', 'c8d', 'gk', 'gv', 'fv', 'yout_a', 'yout_b', 'yout_c']}
    r_in = P.res('in')

    A = Arena(P, 42000)
    vec = A.f32(32)
    r_vec = P.res('vec')
    P.dma('sp', [(vec, vecm)], r_vec)
    cst = A.f32(8)
    r_cst = P.res('cst')
    P.memset('dve', cst[:, 0:1], 1.0, [r_cst])
    P.memset('dve', cst[:, 1:2], 1e-6, [r_cst])
    negFT = A.f32(256).rearrange("p (b h) -> p b h", h=4)
    r_negFT = P.res('negFT')
    mark = A.off

    wfm_sb = A.bf16(8 * NFM).rearrange("p (k c) -> p k c", k=8)
    wtm_sb = A.bf16(8 * NTM).rearrange("p (k c) -> p k c", k=8)
    r_wfm, r_wtm = P.res('wfm'), P.res('wtm')
    P.dma('pool', [(wfm_sb[:, k, :], wfm[k]) for k in range(8)], r_wfm)
    P.dma('pool', [(wtm_sb[:, k, :], wtm[k]) for k in range(8)], r_wtm)
    xbs = [(A.bf16(8 * 512).rearrange("p (k n) -> p k n", k=8), P.res('xb')) for _ in range(2)]
    stf = [(A.f32(6 * 512).rearrange("p (k n) -> p k n", k=6), P.res('stf')) for _ in range(2)]
    stb = [(A.bf16(5 * 512).rearrange("p (k n) -> p k n", k=5), P.res('stb')) for _ in range(2)]
    stl = [(A.bf16(512), P.res('stl')) for _ in range(2)]
    stff = [(A.f32(512), P.res('stff')) for _ in range(2)]
    stk = [(A.f32(4 * 128).rearrange("p (b f) -> p b f", b=4), P.res('stk')) for _ in range(2)]
    stv = [(A.bf16(4 * 512).rearrange("p (b f) -> p b f", b=4), P.res('stv')) for _ in range(2)]
    pi = 0
    for t in range(S // 512):
        ts_ = slice(t * 512, (t + 1) * 512)
        xb, r_xb = xbs[t % 2]
        P.dma('sp', [(xb, x1b.rearrange("(k p) n -> p k n", p=128)[:, :, ts_])], r_xb, reads=[r_in])
        sf, r_sf = stf[t % 2]
        sb, r_sb = stb[t % 2]
        sl_, r_sl = stl[t % 2]
        sff, r_sff = stff[t % 2]
        jobs = [('ax', 0, 128, 'f', sf[:, 0, :], r_sf), ('ax', 128, 128, 'f', sf[:, 1, :], r_sf),
                ('ay', 0, 128, 'f', sf[:, 2, :], r_sf), ('ay', 128, 128, 'f', sf[:, 3, :], r_sf),
                ('gr', 0, 128, 'silu', sf[:, 4, :], r_sf), ('gr', 128, 128, 'silu', sf[:, 5, :], r_sf),
                ('gq', 0, 128, 'q', sb[:, 0, :], r_sb),
                ('fq', 0, 128, 'b', sb[:, 1, :], r_sb), ('fq', 128, 128, 'b', sb[:, 2, :], r_sb),
                ('fk', 0, 128, 'b', sb[:, 3, :], r_sb), ('fk', 128, 128, 'b', sb[:, 4, :], r_sb),
                ('glow', 0, 16, 'b', sl_[0:16, :], r_sl), ('ff', 0, 4, 'f', sff[0:4, :], r_sff)]
        for (nm, co, ncol, kind, dest, r_dest) in jobs:
            ps, r_ps = pst[pi % 8], psr[pi % 8]
            pi += 1
            c0 = FM_OFF[nm] + co
            for k in range(8):
                P.mm(ps[0:ncol, :], wfm_sb[:, k, c0:c0 + ncol], xb[:, k, :], k == 0, k == 7, [r_wfm, r_xb], [r_ps])
            if kind == 'silu':
                P.act(dest, ps[0:ncol, :], AF.Silu, [r_ps], [r_dest])
            elif kind == 'q':
                P.act(dest, ps[0:ncol, :], AF.Copy, [r_ps], [r_dest], scale=0.125)
            elif kind == 'f':
                P.copy('dve', dest, ps[0:ncol, :], [r_ps], [r_dest])
            else:
                P.copy('act', dest, ps[0:ncol, :], [r_ps], [r_dest])
        P.dma('act', [(axT.rearrange("(c p) n -> p c n", p=128)[:, :, ts_], sf[:, 0:2, :])], R['axT'], reads=[r_sf])
        P.dma('act', [(ayT.rearrange("(c p) n -> p c n", p=128)[:, :, ts_], sf[:, 2:4, :])], R['ayT'], reads=[r_sf])
        P.dma('act', [(srT.rearrange("(c p) n -> p c n", p=128)[:, :, ts_], sf[:, 4:6, :])], R['srT'], reads=[r_sf])
        P.dma('act', [(gqT[:, ts_], sb[:, 0, :])], R['gqT'], reads=[r_sb])
        P.dma('act', [(fqT.rearrange("(c p) n -> p c n", p=128)[:, :, ts_], sb[:, 1:3, :])], R['fqT'], reads=[r_sb])
        P.dma('act', [(fkT.rearrange("(c p) n -> p c n", p=128)[:, :, ts_], sb[:, 3:5, :])], R['fkT'], reads=[r_sb])
        P.dma('act', [(lowT[:, ts_], sl_[0:16, :])], R['lowT'], reads=[r_sl])
        P.dma('act', [(fT[:, ts_], sff[0:4, :])], R['fT'], reads=[r_sff])
        sk, r_sk = stk[t % 2]
        sv, r_sv = stv[t % 2]
        for bl in range(4):
            ps, r_ps = pst[pi % 8], psr[pi % 8]
            pi += 1
            ps2, r_ps2 = pst[pi % 8], psr[pi % 8]
            pi += 1
            for k in range(8):
                P.mm(ps[:, :], xb[:, k, bl * 128:(bl + 1) * 128], wtm_sb[:, k, 0:512], k == 0, k == 7, [r_wtm, r_xb], [r_ps])
            for k in range(8):
                P.mm(ps2[:, 0:128], xb[:, k, bl * 128:(bl + 1) * 128], wtm_sb[:, k, 512:640], k == 0, k == 7, [r_wtm, r_xb], [r_ps2])
            ev = 'dve' if bl % 2 == 0 else 'act'
            P.copy(ev, sk[:, bl, :], ps[:, 0:128], [r_ps], [r_sk])
            P.copy(ev, sv[:, bl, 0:256], ps[:, 128:384], [r_ps], [r_sv])
            P.copy(ev, sv[:, bl, 256:384], ps[:, 384:512], [r_ps], [r_sv])
            P.copy(ev, sv[:, bl, 384:512], ps2[:, 0:128], [r_ps2], [r_sv])
        P.dma('act', [(gk_tm[ts_, :].rearrange("(b p) f -> p b f", p=128), sk)], R['gk'], reads=[r_sk])
        P.dma('act', [(gv_tm[ts_, :].rearrange("(b p) f -> p b f", p=128), sv[:, :, 0:256])], R['gv'], reads=[r_sv])
        P.dma('act', [(fv_tm[ts_, :].rearrange("(b p) f -> p b f", p=128), sv[:, :, 256:512])], R['fv'], reads=[r_sv])
    P.barrier()

    if stop == 'B1':
        P.emit(final=[])
        return nc
    A.off = mark
    fs = A.f32(S)
    r_fs = P.res('fs')
    es = A.f32(S)
    r_es = P.res('es')
    c8s = A.bf16(S)
    r_c8s = P.res('c8s')
    ident = A.f32(4)
    r_id = P.res('ident')
    nbf = A.f32(1)
    r_nbf = P.res('nbf')
    P.dma('sp', [(fs[0:4, :], fT)], r_fs, reads=[R['fT']])
    P.ts('dve', nbf[0:4, :], vec[0:4, 18:19], -1.0, ALU.mult, [r_vec], [r_nbf])
    P.act(es[0:4, :], fs[0:4, :], AF.Exp, [r_fs, r_nbf], [r_es], scale=-1.0, bias=nbf[0:4, 0:1])
    P.act(fs[0:4, :], es[0:4, :], AF.Ln, [r_es, r_cst], [r_fs], bias=cst[0:4, 0:1])
    P.op('dve', lambda eng: eng.tensor_tensor_scan(out=es[0:4, :], data0=cst[0:4, 0:1].to_broadcast([4, S]), data1=fs[0:4, :],
                                                   initial=0.0, op0=ALU.mult, op1=ALU.add), [r_fs, r_cst], [r_es])
    P.ts('dve', c8s[0:4, :], es[0:4, :], -8.0, ALU.mult, [r_es], [r_c8s])
    P.dma('act', [(c8d, c8s[0:4, :])], R['c8d'], reads=[r_c8s])
    P.memset('dve', ident[0:4, 0:4], 0.0, [r_id])
    P.op('pool', lambda eng: eng.affine_select(out=ident[0:4, 0:4], in_=ident[0:4, 0:4], pattern=[[-1, 4]],
                                               compare_op=ALU.not_equal, fill=1.0, base=0, channel_multiplier=1), [r_id], [r_id])
    for blk in range(64):
        P.op('pe', lambda eng, blk=blk: eng.transpose(out=pst[0][:, blk * 4:(blk + 1) * 4], in_=es[0:4, blk * 128:(blk + 1) * 128],
                                                      identity=ident[0:4, 0:4]), [r_es, r_id], [psr[0]])
    P.copy('dve', negFT, pst[0][:, 0:256].rearrange("p (b h) -> p b h", h=4), [psr[0]], [r_negFT])
    P.barrier()

    if stop == 'F':
        P.emit(final=[])
        return nc
    A.off = mark
    TL = 2048
    wbd = A.bf16(4 * 128).rearrange("p (a c m) -> p a c m", a=2, c=2)
    r_wbd = P.res('wbd')
    P.memset('dve', wbd, 0.0, [r_wbd])
    parts = []
    for a in range(2):
        for c in range(2):
            for q in range(2):
                parts.append((wbd[q * 64:(q + 1) * 64, a, c, q * 64:(q + 1) * 64], lruw[a, 2 * c + q]))
    P.dma('pool', parts, r_wbd)
    nc8 = A.f32(2)
    r_nc8 = P.res('nc8')
    etmp = A.f32(2)
    r_etmp = P.res('etmp')
    P.act(etmp, vec[:, 14:16], AF.Exp, [r_vec], [r_etmp], scale=-1.0)
    P.act(etmp, etmp, AF.Ln, [r_etmp, r_cst], [r_etmp], bias=cst[:, 0:1])
    P.ts('dve', nc8, etmp, -8.0, ALU.mult, [r_etmp], [r_nc8])
    axb = A.f32(TL + 4)
    names = ['ay', 'u', 'rr', 'ii', 'aa', 'a2', 'bb', 'hh', 'ga']
    B_ = {n: A.f32(TL) for n in names}
    RB = {n: P.res(n) for n in names + ['axb', 'ubf', 'ya']}
    ubf = A.bf16(TL)
    yab = A.bf16(TL)
    for c in range(2):
        for t in range(S // TL):
            t0 = t * TL
            if t == 0:
                P.memset('dve', axb[:, 0:4], 0.0, [RB['axb']])
                P.dma('sp', [(axb[:, 4:4 + TL], axT[c * 128:(c + 1) * 128, 0:TL])], RB['axb'], reads=[R['axT']])
            else:
                P.dma('sp', [(axb[:, 0:4 + TL], axT[c * 128:(c + 1) * 128, t0 - 4:t0 + TL])], RB['axb'], reads=[R['axT']])
            P.dma('sp', [(B_['ay'], ayT[c * 128:(c + 1) * 128, t0:t0 + TL])], RB['ay'], reads=[R['ayT']])
            u = B_['u']
            P.act(u, axb[:, 4:4 + TL], AF.Identity, [RB['axb'], r_vec], [RB['u']], scale=vec[:, 6 + c:7 + c], bias=vec[:, 8 + c:9 + c])
            for k in range(3):
                P.stt('dve', u, axb[:, 1 + k:1 + k + TL], vec[:, 2 * k + c:2 * k + c + 1], u, ALU.mult, ALU.add, [RB['axb'], RB['u'], r_vec], [RB['u']])
            P.copy('pool', ubf, u, [RB['u']], [RB['ubf']])
            for pc in range(TL // 512):
                cs = slice(pc * 512, (pc + 1) * 512)
                P.mm(pst[1][:, :], wbd[:, 0, c, :], ubf[:, cs], True, True, [r_wbd, RB['ubf']], [psr[1]])
                P.mm(pst[2][:, :], wbd[:, 1, c, :], ubf[:, cs], True, True, [r_wbd, RB['ubf']], [psr[2]])
                P.act(B_['rr'][:, cs], pst[1][:, :], AF.Sigmoid, [psr[1], r_vec], [RB['rr']], bias=vec[:, 10 + c:11 + c])
                P.act(B_['ii'][:, cs], pst[2][:, :], AF.Sigmoid, [psr[2], r_vec], [RB['ii']], bias=vec[:, 12 + c:13 + c])
            P.act(B_['aa'], B_['rr'], AF.Exp, [RB['rr'], r_nc8], [RB['aa']], scale=nc8[:, c:c + 1])
            P.tt('pool', B_['a2'], B_['aa'], B_['aa'], ALU.mult, [RB['aa']], [RB['a2']])
            P.act(B_['a2'], B_['a2'], AF.Sqrt, [RB['a2'], r_cst], [RB['a2']], scale=-1.0, bias=cst[:, 0:1])
            P.tt('dve', B_['bb'], B_['ii'], u, ALU.mult, [RB['ii'], RB['u']], [RB['bb']])
            P.tt('pool', B_['bb'], B_['bb'], B_['a2'], ALU.mult, [RB['bb'], RB['a2']], [RB['bb']])
            if t == 0:
                P.op('dve', lambda eng: eng.tensor_tensor_scan(out=B_['hh'], data0=B_['aa'], data1=B_['bb'], initial=0.0,
                                                               op0=ALU.mult, op1=ALU.add), [RB['aa'], RB['bb']], [RB['hh']])
            else:
                hl = A.f32(1)
                r_hl = P.res('hl')
                P.copy('dve', hl, B_['hh'][:, TL - 1:TL], [RB['hh']], [r_hl])
                P.op('dve', lambda eng, hl=hl: eng.tensor_tensor_scan(out=B_['hh'], data0=B_['aa'], data1=B_['bb'], initial=hl[:, 0:1],
                                                                      op0=ALU.mult, op1=ALU.add), [RB['aa'], RB['bb'], r_hl], [RB['hh']])
            P.act(B_['ga'], B_['ay'], AF.Gelu_apprx_tanh, [RB['ay']], [RB['ga']])
            P.tt('dve', yab, B_['ga'], B_['hh'], ALU.mult, [RB['ga'], RB['hh']], [RB['ya']])
            P.dma('act', [(yout[c * 128:(c + 1) * 128, t0:t0 + TL], yab)], R['yout_a'], reads=[RB['ya']])
    P.barrier()

    if stop == 'LRU':
        P.emit(final=[])
        return nc
    A.off = mark
    Sst = A.f32(256)
    r_S = P.res('S')
    P.memset('dve', Sst, 0.0, [r_S])
    Sbf = [(A.bf16(256), P.res('Sbf')) for _ in range(2)]
    U = A.f32(64)
    r_U = P.res('U')
    P.memset('dve', U[0:64, :], 1.0, [r_U])
    P.op('pool', lambda eng: eng.affine_select(out=U[0:64, :], in_=U[0:64, :], pattern=[[-1, 64]], compare_op=ALU.is_ge,
                                               fill=0.0, base=-1, channel_multiplier=1), [r_U], [r_U])
    ones2 = A.f32(2)
    P.memset('dve', ones2, 1.0, [r_U])
    ones128 = A.bf16(128)
    P.memset('dve', ones128, 1.0 / 128.0, [r_U])
    wg2e = A.bf16(128)
    r_wg2e = P.res('wg2e')
    P.memset('dve', wg2e[0:64, :], 0.0, [r_wg2e])
    P.dma('pool', [(wg2e[0:16, :], wg2), (wg2e[32:33, :], bg)], r_wg2e)
    lowe = [(A.bf16(512), P.res('lowe')) for _ in range(2)]
    for (lw, r_lw) in lowe:
        P.memset('dve', lw[0:32, :], 0.0, [r_lw])
        P.memset('dve', lw[32:33, :], 1.0, [r_lw])
    kts = [(A.f32(8 * 128).rearrange("p (c f) -> p c f", c=8), P.res('kt')) for _ in range(2)]
    vts = [(A.bf16(8 * 256).rearrange("p (c f) -> p c f", c=8), P.res('vt')) for _ in range(2)]
    qts = [(A.bf16(512), P.res('qt')) for _ in range(2)]
    srs = [(A.f32(2 * 512).rearrange("p (h n) -> p h n", h=2), P.res('sr')) for _ in range(2)]
    ee = A.f32(8 * 128).rearrange("p (c f) -> p c f", c=8)
    lsb = A.f32(8 * 128).rearrange("p (c f) -> p c f", c=8)
    ed = A.f32(8 * 128).rearrange("p (c f) -> p c f", c=8)
    kdec = A.bf16(8 * 128).rearrange("p (c f) -> p c f", c=8)
    dec = A.f32(16)
    osq = A.bf16(512)
    sdg = A.f32(512)
    rsg = A.f32(512)
    t1g = A.f32(512)
    ybs = [(A.bf16(512), P.res('yb')) for _ in range(2)]
    RG = {n: P.res(n) for n in ['ee', 'lsb', 'ed', 'kdec', 'dec', 'osq', 'sdg', 'rsg', 't1g']}
    yi = 0
    for t in range(S // 512):
        ts_ = slice(t * 512, (t + 1) * 512)
        kt, r_kt = kts[t % 2]
        vt, r_vt = vts[t % 2]
        qt, r_qt = qts[t % 2]
        sr, r_sr = srs[t % 2]
        lw, r_lw = lowe[t % 2]
        P.dma('sp', [(kt[0:64], gk_tm[ts_, :].rearrange("(c t) f -> t c f", t=64))], r_kt, reads=[R['gk']])
        P.dma('sp', [(vt[0:64], gv_tm[ts_, :].rearrange("(c t) f -> t c f", t=64))], r_vt, reads=[R['gv']])
        P.dma('sp', [(qt, gqT[:, ts_])], r_qt, reads=[R['gqT']])
        P.dma('sp', [(sr, srT.rearrange("(h p) n -> p h n", p=128)[:, :, ts_])], r_sr, reads=[R['srT']])
        P.dma('sp', [(lw[0:16, :], lowT[:, ts_])], r_lw, reads=[R['lowT']])
        for c in range(8):
            P.mm(pst[c // 4][0:64, (c % 4) * 128:(c % 4 + 1) * 128], lw[0:33, c * 64:(c + 1) * 64], wg2e[0:33, :], True, True,
                 [r_lw, r_wg2e], [psr[c // 4]])
        for hf in range(2):
            P.act(ee[0:64, hf * 4:(hf + 1) * 4, :], pst[hf][0:64, :].rearrange("p (c f) -> p c f", c=4), AF.Exp, [psr[hf]], [RG['ee']], scale=-1.0)
        P.act(lsb[0:64], ee[0:64], AF.Ln, [RG['ee'], r_cst], [RG['lsb']], bias=cst[0:64, 0:1])
        for c in range(8):
            P.mm(pst[c // 4][0:64, (c % 4) * 128:(c % 4 + 1) * 128], U[0:64, 0:64], lsb[0:64, c, :], True, True,
                 [r_U, RG['lsb']], [psr[c // 4]])
        for c in range(8):
            P.mm(pst[2][:, c * 2:(c + 1) * 2], lsb[0:64, c, :], ones2[0:64, 0:2], True, True, [r_U, RG['lsb']], [psr[2]])
        P.act(dec, pst[2][:, 0:16], AF.Exp, [psr[2]], [RG['dec']], scale=-1.0 / 16.0)
        for hf in range(2):
            P.act(ed[0:64, hf * 4:(hf + 1) * 4, :], pst[hf][0:64, :].rearrange("p (c f) -> p c f", c=4), AF.Exp, [psr[hf]], [RG['ed']], scale=-1.0 / 16.0)
        P.tt('dve', kdec[0:64], kt[0:64], ed[0:64], ALU.mult, [r_kt, RG['ed']], [RG['kdec']])
        for c in range(8):
            dps, r_dps = pst[3 + c % 2], psr[3 + c % 2]
            P.mm(dps[:, 0:256], kdec[0:64, c, :], vt[0:64, c, :], True, True, [RG['kdec'], r_vt], [r_dps])
            P.stt('dve', Sst, Sst, dec[:, 2 * c:2 * c + 1], dps[:, 0:256], ALU.mult, ALU.add, [r_S, RG['dec'], r_dps], [r_S])
            sb_, r_sb_ = Sbf[c % 2]
            P.copy('act', sb_, Sst, [r_S], [r_sb_])
            for h in range(2):
                P.mm(pst[5 + h][:, c * 64:(c + 1) * 64], sb_[64 * h:64 * h + 64, 128 * h:128 * h + 128],
                     qt[64 * h:64 * h + 64, c * 64:(c + 1) * 64], True, True, [r_sb_, r_qt], [psr[5 + h]])
        for h in range(2):
            P.act(osq, pst[5 + h][:, :], AF.Square, [psr[5 + h]], [RG['osq']])
            P.mm(pst[7][:, :], ones128, osq, True, True, [r_U, RG['osq']], [psr[7]])
            P.act(sdg, pst[7][:, :], AF.Sqrt, [psr[7], r_cst], [RG['sdg']], bias=cst[:, 1:2])
            P.op('dve', lambda eng: eng.reciprocal(out=rsg, in_=sdg), [RG['sdg']], [RG['rsg']])
            P.tt('dve', t1g, pst[5 + h][:, :], rsg, ALU.mult, [psr[5 + h], RG['rsg']], [RG['t1g']])
            yb, r_yb = ybs[yi % 2]
            yi += 1
            P.stt('dve', yb, t1g, vec[:, 16 + h:17 + h], sr[:, h, :], ALU.mult, ALU.mult, [RG['t1g'], r_vec, r_sr], [r_yb])
            P.dma('act', [(yout[256 + h * 128:256 + (h + 1) * 128, ts_], yb)], R['yout_b'], reads=[r_yb])
    P.barrier()

    if stop == 'GLA':
        P.emit(final=[])
        return nc
    A.off = mark
    onesf = A.f32(64)
    r_onesf = P.res('onesf')
    P.memset('dve', onesf, 1.0, [r_onesf])
    qes = [(A.bf16(S), P.res('qe')) for _ in range(2)]
    kes = [(A.bf16(S), P.res('ke')) for _ in range(2)]
    ves = [(A.bf16(64 * 65).rearrange("p (b d) -> p b d", d=65), P.res('ve')) for _ in range(2)]
    for i in range(2):
        P.memset('dve', kes[i][0][64:65, :], 1.0, [kes[i][1]])
        P.memset('dve', ves[i][0][:, :, 64:65], 1.0, [ves[i][1]])
    pts = [(A.bf16(512), P.res('pt')) for _ in range(4)]
    rden = A.f32(512)
    r_rden = P.res('rden')
    num = A.f32(512)
    r_num = P.res('num')
    ycs = [(A.bf16(512), P.res('yc')) for _ in range(2)]
    si = 0
    oi = 0
    for h in range(4):
        qe, r_qe = qes[h % 2]
        ke, r_ke = kes[h % 2]
        ve, r_ve = ves[h % 2]
        P.dma('sp', [(qe[0:64, :], fqT[h * 64:(h + 1) * 64, :]), (qe[64:65, :], c8d[h:h + 1, :])], r_qe, reads=[R['fqT'], R['c8d']])
        P.dma('sp', [(ke[0:64, :], fkT[h * 64:(h + 1) * 64, :])], r_ke, reads=[R['fkT']])
        fvv = fv_tm[:, h * 64:(h + 1) * 64].rearrange("(b p) d -> p b d", p=128)
        P.dma('sp', [(ve[:, 8 * i:8 * i + 8, 0:64], fvv[:, 8 * i:8 * i + 8, :]) for i in range(8)], r_ve, reads=[R['fv']])
        for qi in range(S // 512):
            ops_, r_ops = pst[4 + oi % 2], psr[4 + oi % 2]
            oi += 1
            nkb = 4 * qi + 4
            for kb in range(nkb):
                d = kb - 4 * qi
                j0 = max(d, 0) * 128
                sps, r_sps = pst[si % 4], psr[si % 4]
                pt, r_pt = pts[si % 4]
                si += 1
                P.mm(sps[:, j0:512], ke[0:65, kb * 128:(kb + 1) * 128], qe[0:65, qi * 512 + j0:(qi + 1) * 512], True, True,
                     [r_ke, r_qe], [r_sps])
                P.act(pt[:, j0:512], sps[:, j0:512], AF.Exp, [r_sps, r_negFT], [r_pt], scale=0.125, bias=negFT[:, kb, h:h + 1])
                if d >= 0:
                    P.op('pool', lambda eng, pt=pt, j0=j0: eng.affine_select(out=pt[:, j0:j0 + 128], in_=pt[:, j0:j0 + 128], pattern=[[1, 128]],
                                                                             compare_op=ALU.is_ge, fill=0.0, base=0, channel_multiplier=-1),
                         [r_pt], [r_pt])
                P.mm(ops_[0:65, j0:512], ve[:, kb, 0:65], pt[:, j0:512], kb == 0, kb == nkb - 1, [r_ve, r_pt], [r_ops])
            P.op('dve', lambda eng, ops_=ops_: eng.reciprocal(out=rden[64:65, :], in_=ops_[64:65, :]), [r_ops], [r_rden])
            bps, r_bps = pst[6 + oi % 2], psr[6 + oi % 2]
            P.mm(bps[0:64, :], onesf[64:65, 0:64], rden[64:65, :], True, True, [r_onesf, r_rden], [r_bps])
            P.copy('act', num[0:64, :], ops_[0:64, :], [r_ops, r_rden], [r_num])
            yc, r_yc = ycs[oi % 2]
            P.tt('dve', yc[0:64, :], num[0:64, :], bps[0:64, :], ALU.mult, [r_num, r_bps], [r_yc])
            P.dma('act', [(yout[512 + h * 64:512 + (h + 1) * 64, qi * 512:(qi + 1) * 512], yc[0:64, :])], R['yout_c'], reads=[r_yc])
    P.emit(final=[R['yout_a'], R['yout_b'], R['yout_c']])
    return nc


def run_B(inp, l, x1b_full_cores, stop=None):
    nc = build_B(stop)
    in_maps = []
    n = len(x1b_full_cores)
    for c in range(n):
        m = mixer_host(inp, l, c % 2)
        m["x1b"] = x1b_full_cores[c]
        in_maps.append(m)
    res = run_bass_kernel_spmd(nc, in_maps, core_ids=list(range(n)))
    return [r["yout"] for r in res.results]


def kernel(**inp):
    inp = {k: np.asarray(v) for k, v in inp.items()}
    x = inp['x']
    n = 8
    tsl = lambda c: slice((c % 2) * TLOC, (c % 2 + 1) * TLOC)
    xT = [np.ascontiguousarray(x[c // 2, tsl(c), :].T) for c in range(n)]
    x1f, x1b = run_A(inp, 0, xT)
    xo = None
    for l in range(DEPTH):
        x1b_full = [np.ascontiguousarray(np.concatenate([x1b[2 * b], x1b[2 * b + 1]], axis=1)) for b in range(NB)]
        ys = run_B(inp, l, [x1b_full[c // 2] for c in range(n)])
        yin = []
        for c in range(n):
            b = c // 2
            y0, y1 = ys[2 * b], ys[2 * b + 1]
            parts = []
            for off in (0, 256, 512):
                parts.append(y0[off:off + 256, tsl(c)])
                parts.append(y1[off:off + 256, tsl(c)])
            yin.append(np.ascontiguousarray(np.concatenate(parts, axis=0)))
        pT = [np.ascontiguousarray(inp['p'][l][c // 2, tsl(c), :].T) for c in range(n)]
        if l + 1 < DEPTH:
            x1f, x1b = run_C(inp, l, x1f, x1b, yin, pT, True)
        else:
            xo = run_C(inp, l, x1f, x1b, yin, pT, False)
    out = np.zeros((NB, SEQ, D), np.float32)
    for c in range(n):
        out[c // 2, tsl(c), :] = xo[c].T
    return out
```
